# Optimizing a Trainium2 kernel written in Bass

```python
import jax, jax.numpy as jnp
from jax import lax
import numpy as np

D_MODEL = 2048
BATCH = 1
SEQ = 8192
DEPTH = 2

GRID_W = 64
CTX_LEN = 256
HEAD_DIM = 128
N_BRANCH = 4
BRANCH_W = D_MODEL // N_BRANCH
A_HEADS = BRANCH_W // HEAD_DIM
A_KV_HEADS = A_HEADS // 2
KV_W = A_KV_HEADS * HEAD_DIM
ROPE_THETA = 10000.0
ROPE_HALF = HEAD_DIM // 4
Q_BLOCK = 128
B_GROUPS = 4
B_GROUP_W = BRANCH_W // B_GROUPS
POOL_WINDOWS = (2, 4, 8, 16)
C_GROUPS = 4
C_GROUP_W = BRANCH_W // C_GROUPS
C_CHUNK = 128
D_HEADS = BRANCH_W // HEAD_DIM
NA_WIN_H = 8
NA_WIN_W = 16
FFN_HIDDEN = -(-(8 * D_MODEL) // (3 * 256)) * 256
EPS = 1e-6

A_Q0 = 0
A_K0 = A_Q0 + A_HEADS * HEAD_DIM
A_V0 = A_K0 + KV_W
A_END = A_V0 + KV_W
B0 = A_END
B_END = B0 + BRANCH_W
C_U0 = B_END
C_V0 = C_U0 + BRANCH_W
C_END = C_V0 + BRANCH_W
D_Q0 = C_END
D_K0 = D_Q0 + D_HEADS * HEAD_DIM
D_V0 = D_K0 + D_HEADS * HEAD_DIM
D_END = D_V0 + D_HEADS * HEAD_DIM
G0 = D_END
IN_COLS = G0 + N_BRANCH * D_MODEL

kernel_name = 'hybrid_dit_parallel_gated_mixers'


def _rms(x, g):
    xf = x.astype(jnp.float32)
    y = xf * lax.rsqrt(jnp.mean(xf * xf, axis=-1, keepdims=True) + EPS)
    return (y * g.astype(jnp.float32)).astype(x.dtype)


def _layernorm(x, g, b):
    xf = x.astype(jnp.float32)
    mu = jnp.mean(xf, axis=-1, keepdims=True)
    var = jnp.mean(jnp.square(xf - mu), axis=-1, keepdims=True)
    y = (xf - mu) * lax.rsqrt(var + EPS) * g.astype(jnp.float32) + b.astype(jnp.float32)
    return y.astype(x.dtype)


def _modulate(h, shift, scale):
    return h * (1.0 + scale) + shift


def _heads(a):
    return a.reshape(a.shape[:-1] + (-1, HEAD_DIM))


def _rope_tables(n):
    t = jnp.arange(n)
    row = (t // GRID_W).astype(jnp.float32)
    col = (t % GRID_W).astype(jnp.float32)
    inv = ROPE_THETA ** (-jnp.arange(ROPE_HALF, dtype=jnp.float32) / ROPE_HALF)
    ang = jnp.concatenate([row[:, None] * inv, col[:, None] * inv], axis=-1)
    return jnp.cos(ang), jnp.sin(ang)


def _apply_rope(x, cos, sin):
    b, n, h, d = x.shape
    xr = x.reshape(b, n, h, 2, 2, ROPE_HALF)
    x1, x2 = xr[..., 0, :], xr[..., 1, :]
    cs = cos.reshape(n, 2, ROPE_HALF)[None, :, None].astype(x.dtype)
    sn = sin.reshape(n, 2, ROPE_HALF)[None, :, None].astype(x.dtype)
    return jnp.stack([x1 * cs - x2 * sn, x2 * cs + x1 * sn], axis=-2).reshape(b, n, h, d)


def _block_attention(q, k, v):
    b, n, h, d = q.shape
    hkv = k.shape[2]
    g = h // hkv
    qb = q.reshape(b, n // Q_BLOCK, Q_BLOCK, hkv, g, d).transpose(1, 0, 2, 3, 4, 5)
    scale = d ** -0.5

    def one(qblk):
        s = jnp.einsum('bqkgd,bskd->bkgqs', qblk, k).astype(jnp.float32) * scale
        p = jax.nn.softmax(s, axis=-1).astype(v.dtype)
        return jnp.einsum('bkgqs,bskd->bqkgd', p, v)

    o = lax.map(one, qb)
    return o.transpose(1, 0, 2, 3, 4, 5).reshape(b, n, h * d)


def _pool_mix(p, w, scale):
    b, n, _ = p.shape
    pg = p.reshape(b, n, B_GROUPS, B_GROUP_W).astype(jnp.float32)
    cs = jnp.concatenate([jnp.zeros_like(pg[:, :1]), jnp.cumsum(pg, axis=1)], axis=1)
    t = jnp.arange(n)[:, None]
    half = jnp.array(POOL_WINDOWS, jnp.int32)[None, :] // 2
    lo = jnp.clip(t - half, 0, n)
    hi = jnp.clip(t + half, 0, n)
    gidx = jnp.arange(B_GROUPS)[None, :]
    win = cs[:, hi, gidx] - cs[:, lo, gidx]
    mean = win / (hi - lo).astype(jnp.float32)[None, :, :, None]
    dlt = (mean - pg).astype(p.dtype)
    y = jnp.einsum('bngc,gce->bnge', dlt, w).reshape(b, n, BRANCH_W)
    return y * scale


def _chunk_mlp(u, v, g, bn, ws, bs):
    b, n, _ = v.shape
    vn = _layernorm(v, g, bn).reshape(b, n // C_CHUNK, C_CHUNK, C_GROUPS, C_GROUP_W)
    mixed = jnp.einsum('gpq,bkqgc->bkpgc', ws, vn) + bs.T[None, None, :, :, None]
    return u * mixed.reshape(b, n, BRANCH_W)


def _neighbourhood_attention(q, k, v, k_ctx, v_ctx, rpb, rows):
    b, n, h, d = q.shape
    wh = min(NA_WIN_H, rows)
    qg = q.reshape(b, rows, GRID_W, h, d)
    kg = k.reshape(b, rows, GRID_W, h, d)
    vg = v.reshape(b, rows, GRID_W, h, d)
    r = jnp.arange(rows)
    r0 = jnp.clip(r - wh // 2, 0, rows - wh)
    dr_idx = r0[:, None] + jnp.arange(wh)[None, :] - r[:, None] + (NA_WIN_H - 1)
    cc = jnp.arange(GRID_W)
    c0 = jnp.clip(cc - NA_WIN_W // 2, 0, GRID_W - NA_WIN_W)
    col_idx = c0[:, None] + jnp.arange(NA_WIN_W)[None, :]
    dc_idx = col_idx - cc[:, None] + (NA_WIN_W - 1)
    rpb_cols = rpb[:, :, dc_idx]
    scale = d ** -0.5
    n_loc = wh * NA_WIN_W

    def one_row(args):
        q_row, start, dr = args
        k_rows = lax.dynamic_slice_in_dim(kg, start, wh, axis=1)
        v_rows = lax.dynamic_slice_in_dim(vg, start, wh, axis=1)
        k_win = k_rows[:, :, col_idx]
        v_win = v_rows[:, :, col_idx]
        bias = rpb_cols[:, dr].transpose(0, 2, 1, 3).astype(jnp.float32)
        s_loc = jnp.einsum('bchd,bicjhd->bhcij', q_row, k_win).astype(jnp.float32) * scale + bias
        s_ctx = jnp.einsum('bchd,blhd->bhcl', q_row, k_ctx).astype(jnp.float32) * scale
        s = jnp.concatenate([s_loc.reshape(b, h, GRID_W, n_loc), s_ctx], axis=-1)
        p = jax.nn.softmax(s, axis=-1).astype(v.dtype)
        p_loc = p[..., :n_loc].reshape(b, h, GRID_W, wh, NA_WIN_W)
        p_ctx = p[..., n_loc:]
        return (jnp.einsum('bhcij,bicjhd->bchd', p_loc, v_win)
                + jnp.einsum('bhcl,blhd->bchd', p_ctx, v_ctx))

    o = lax.map(one_row, (qg.transpose(1, 0, 2, 3, 4), r0, dr_idx))
    return o.transpose(1, 0, 2, 3, 4).reshape(b, n, h * d)


def _merge(outs, gate_logits, w_br, w_o):
    y = jnp.einsum('bnik,ikd->bnid', outs, w_br)
    gates = jax.nn.sigmoid(gate_logits.reshape(y.shape))
    return jnp.sum(gates * y, axis=2) @ w_o


def _swiglu(h, wg, wu, wd):
    return (jax.nn.silu(h @ wg) * (h @ wu)) @ wd


def setup_inputs(seed: int = 0) -> dict:
    key = jax.random.key(seed)
    ks = jax.random.split(key, 25)
    L, D = DEPTH, D_MODEL

    def nrm(k, shape, s):
        return jax.random.normal(k, shape, jnp.float32) * s

    def gain(k, shape):
        return 1.0 + 0.02 * jax.random.normal(k, shape, jnp.float32)

    return {
        'x': nrm(ks[0], (BATCH, SEQ, D), 1.0),
        'c': nrm(ks[1], (BATCH, D), 1.0),
        'ctx': nrm(ks[2], (BATCH, CTX_LEN, D), 1.0),
        'c_ctx': nrm(ks[3], (D,), 1.0),
        'ada_w': nrm(ks[4], (L, D, 6 * D), 0.5 * D ** -0.5),
        'ada_b': nrm(ks[5], (L, 6 * D), 0.01),
        'norm_pre_mix': gain(ks[6], (L, D)),
        'norm_post_mix': gain(ks[7], (L, D)),
        'norm_pre_ffn': gain(ks[8], (L, D)),
        'norm_post_ffn': gain(ks[9], (L, D)),
        'w_in': nrm(ks[10], (L, D, IN_COLS), D ** -0.5),
        'a_q_norm': gain(ks[11], (L, HEAD_DIM)),
        'a_k_norm': gain(ks[12], (L, HEAD_DIM)),
        'b_w': nrm(ks[13], (L, B_GROUPS, B_GROUP_W, B_GROUP_W), B_GROUP_W ** -0.5),
        'b_scale': gain(ks[14], (L, BRANCH_W)),
        'c_norm_g': gain(ks[15], (L, BRANCH_W)),
        'c_norm_b': nrm(ks[16], (L, BRANCH_W), 0.02),
        'c_ws': nrm(ks[17], (L, C_GROUPS, C_CHUNK, C_CHUNK), C_CHUNK ** -0.5),
        'c_bs': gain(ks[18], (L, C_GROUPS, C_CHUNK)),
        'd_rpb': nrm(ks[19], (L, D_HEADS, 2 * NA_WIN_H - 1, 2 * NA_WIN_W - 1), 0.1),
        'w_br': nrm(ks[20], (L, N_BRANCH, BRANCH_W, D), BRANCH_W ** -0.5),
        'w_o': nrm(ks[21], (L, D, D), D ** -0.5),
        'w_gate': nrm(ks[22], (L, D, FFN_HIDDEN), D ** -0.5),
        'w_up': nrm(ks[23], (L, D, FFN_HIDDEN), D ** -0.5),
        'w_down': nrm(ks[24], (L, FFN_HIDDEN, D), FFN_HIDDEN ** -0.5),
    }


def reference(x, c, ctx, c_ctx, ada_w, ada_b, norm_pre_mix, norm_post_mix, norm_pre_ffn,
              norm_post_ffn, w_in, a_q_norm, a_k_norm, b_w, b_scale, c_norm_g, c_norm_b,
              c_ws, c_bs, d_rpb, w_br, w_o, w_gate, w_up, w_down):
    n = x.shape[1]
    rows = n // GRID_W
    cos, sin = _rope_tables(n)
    xl, xc = x, ctx
    for layer in range(DEPTH):
        last = layer == DEPTH - 1
        w = w_in[layer]
        mod_l = (jax.nn.silu(c) @ ada_w[layer] + ada_b[layer])[:, None, :]
        mod_c = (jax.nn.silu(c_ctx) @ ada_w[layer] + ada_b[layer])[None, None, :]
        sh_l, sc_l, g_l, shf_l, scf_l, gf_l = jnp.split(mod_l, 6, axis=-1)
        sh_c, sc_c, g_c, shf_c, scf_c, gf_c = jnp.split(mod_c, 6, axis=-1)

        hl = _modulate(_rms(xl, norm_pre_mix[layer]), sh_l, sc_l)
        hc = _modulate(_rms(xc, norm_pre_mix[layer]), sh_c, sc_c)

        if last:
            pc_a = hc @ w[:, A_K0:A_END]
            pc_d = hc @ w[:, D_K0:D_END]
        else:
            pc = hc @ w
            pc_a = pc[..., A_K0:A_END]
            pc_d = pc[..., D_K0:D_END]
        ka_c = _rms(_heads(pc_a[..., :KV_W]), a_k_norm[layer])
        va_c = _heads(pc_a[..., KV_W:])
        kd_c = _heads(pc_d[..., :BRANCH_W])
        vd_c = _heads(pc_d[..., BRANCH_W:])

        pl = hl @ w
        qa = _apply_rope(_rms(_heads(pl[..., A_Q0:A_K0]), a_q_norm[layer]), cos, sin)
        ka = _apply_rope(_rms(_heads(pl[..., A_K0:A_V0]), a_k_norm[layer]), cos, sin)
        va = _heads(pl[..., A_V0:A_END])
        out_a = _block_attention(qa, jnp.concatenate([ka, ka_c], axis=1),
                                 jnp.concatenate([va, va_c], axis=1))
        out_b = _pool_mix(pl[..., B0:B_END], b_w[layer], b_scale[layer])
        out_c = _chunk_mlp(pl[..., C_U0:C_V0], pl[..., C_V0:C_END], c_norm_g[layer],
                           c_norm_b[layer], c_ws[layer], c_bs[layer])
        out_d = _neighbourhood_attention(_heads(pl[..., D_Q0:D_K0]), _heads(pl[..., D_K0:D_V0]),
                                         _heads(pl[..., D_V0:D_END]), kd_c, vd_c, d_rpb[layer], rows)
        y_l = _merge(jnp.stack([out_a, out_b, out_c, out_d], axis=2), pl[..., G0:],
                     w_br[layer], w_o[layer])
        xl = xl + g_l * _rms(y_l, norm_post_mix[layer])
        hf_l = _modulate(_rms(xl, norm_pre_ffn[layer]), shf_l, scf_l)
        xl = xl + gf_l * _rms(_swiglu(hf_l, w_gate[layer], w_up[layer], w_down[layer]),
                              norm_post_ffn[layer])

        if not last:
            qa_c = _rms(_heads(pc[..., A_Q0:A_K0]), a_q_norm[layer])
            oa_c = _block_attention(qa_c, ka_c, va_c)
            ob_c = _pool_mix(pc[..., B0:B_END], b_w[layer], b_scale[layer])
            oc_c = _chunk_mlp(pc[..., C_U0:C_V0], pc[..., C_V0:C_END], c_norm_g[layer],
                              c_norm_b[layer], c_ws[layer], c_bs[layer])
            od_c = _block_attention(_heads(pc[..., D_Q0:D_K0]), kd_c, vd_c)
            y_c = _merge(jnp.stack([oa_c, ob_c, oc_c, od_c], axis=2), pc[..., G0:],
                         w_br[layer], w_o[layer])
            xc = xc + g_c * _rms(y_c, norm_post_mix[layer])
            hf_c = _modulate(_rms(xc, norm_pre_ffn[layer]), shf_c, scf_c)
            xc = xc + gf_c * _rms(_swiglu(hf_c, w_gate[layer], w_up[layer], w_down[layer]),
                                  norm_post_ffn[layer])
    return xl
```

```python
import contextlib
import numpy as np
import ml_dtypes
import concourse.bass as bass
import concourse.mybir as mybir
from concourse.bass_utils import run_bass_kernel_spmd

F32 = mybir.dt.float32
BF16 = mybir.dt.bfloat16
ALU = mybir.AluOpType
AF = mybir.ActivationFunctionType
AX = mybir.AxisListType

D = 2048
KC = 16
SEQ = 8192
NCORE = 8
TL = 1024
CTX = 256
GRID_W = 64
FFN = 5632
HC = 44
EPS = 1e-6
NEG = -30000.0
SCALE = 128 ** -0.5

ENGS = ("pe", "act", "dve", "pool", "sp")


class _Op:
    __slots__ = ("eng", "fn", "deps", "dma", "needed", "tok", "stream")

    def __init__(self, eng, fn, dma, stream):
        self.eng = eng
        self.fn = fn
        self.deps = []
        self.dma = dma
        self.stream = stream
        self.needed = False
        self.tok = None


class Sched:
    def __init__(self):
        self.ops = {e: [] for e in ENGS}
        self.last_write = {}
        self.readers = {}
        self.stream_cnt = {}
        self.stream_last = {}
        self.last_op = {e: None for e in ENGS}

    def op(self, eng, fn, reads=(), writes=(), stream=None):
        o = _Op(eng, fn, stream is not None, stream)
        deps = []
        for k in reads:
            lw = self.last_write.get(k)
            if lw is not None:
                deps.append((lw, "raw"))
        for k in writes:
            lw = self.last_write.get(k)
            if lw is not None:
                deps.append((lw, "waw"))
            for r in self.readers.get(k, ()):
                deps.append((r, "war"))
        seen = set()
        for d, kind in deps:
            if d is o or id(d) in seen:
                continue
            if (not d.dma) and (not o.dma) and d.eng == o.eng:
                if o.eng == "pe" or kind != "raw":
                    continue
            seen.add(id(d))
            o.deps.append(d)
        for k in reads:
            self.readers.setdefault(k, []).append(o)
        for k in writes:
            self.last_write[k] = o
            self.readers[k] = []
        if o.dma:
            c = self.stream_cnt.get(stream, 0) + 16
            self.stream_cnt[stream] = c
            o.tok = c
            self.stream_last[stream] = o
        self.ops[eng].append(o)
        self.last_op[eng] = o
        return o

    def barrier(self):
        lasts = [self.last_op[e] for e in ENGS if self.last_op[e] is not None]
        dmas = list(self.stream_last.values())
        for e in ENGS:
            o = _Op(e, None, False, None)
            for d in lasts:
                if d.dma or d.eng != e:
                    o.deps.append(d)
            for d in dmas:
                if d not in o.deps:
                    o.deps.append(d)
            self.ops[e].append(o)
        self.last_write = {}
        self.readers = {}

    def emit(self, nc):
        for e in ENGS:
            for o in self.ops[e]:
                for d in o.deps:
                    if not d.dma:
                        d.needed = True
        for e in ENGS:
            c = 0
            for o in self.ops[e]:
                if o.fn is not None and not o.dma and o.needed:
                    c += 1
                    o.tok = c
        with contextlib.ExitStack() as es:
            esem = {e: es.enter_context(nc.semaphore("s_" + e)) for e in ENGS}
            ssem = {s: es.enter_context(nc.semaphore("d_%d" % i))
                    for i, s in enumerate(self.stream_cnt)}
            block = es.enter_context(nc.Block())
            stream_cnt = self.stream_cnt
            allops = self.ops

            def body_for(e):
                def body(eng):
                    seen = {}
                    for o in allops[e]:
                        waits = {}
                        for d in o.deps:
                            if d.dma:
                                key = ("s", d.stream)
                                sem = ssem[d.stream]
                            else:
                                key = ("e", d.eng)
                                sem = esem[d.eng]
                            v = d.tok
                            if seen.get(key, 0) >= v:
                                continue
                            if key not in waits or waits[key][1] < v:
                                waits[key] = (sem, v)
                        for key, (sem, v) in waits.items():
                            eng.wait_ge(sem, v)
                            seen[key] = v
                        if o.fn is None:
                            continue
                        ins = o.fn(eng)
                        if o.dma:
                            ins.then_inc(ssem[o.stream], 16)
                        elif o.needed:
                            ins.then_inc(esem[e], 1)
                    if e == "sp":
                        for s, c in stream_cnt.items():
                            if seen.get(("s", s), 0) < c:
                                eng.wait_ge(ssem[s], c)
                return body

            block.tensor(body_for("pe"))
            block.scalar(body_for("act"))
            block.vector(body_for("dve"))
            block.gpsimd(body_for("pool"))
            block.sync(body_for("sp"))


VEC_COLS = dict(c=(0, 16), cctx=(16, 32), ada_b=(32, 128), npm=(128, 144), npo=(144, 160),
                npf=(160, 176), npof=(176, 192), aqn=(192, 193), akn=(193, 194), bsc=(194, 198))
NVEC = 198


class Prog:
    def __init__(self, stages):
        self.stages = stages
        self.nc = bass.Bass("TRN2", target_bir_lowering=False)
        self.S = Sched()
        self.es = contextlib.ExitStack()
        self.din = {}
        self.dout = {}
        self.wrot = 0
        self.accrot = 0
        self.uid = 0

    def inp(self, name, shape, dt=F32):
        if name not in self.din:
            self.din[name] = self.nc.dram_tensor(name, list(shape), dt, kind="ExternalInput").ap()
        return self.din[name]

    def outp(self, name, shape, dt=F32):
        if name not in self.dout:
            self.dout[name] = self.nc.dram_tensor(name, list(shape), dt, kind="ExternalOutput").ap()
        return self.dout[name]

    def sb(self, name, shape, dt):
        return self.es.enter_context(self.nc.sbuf_tensor("sb_" + name, list(shape), dt))

    def mm(self, out, lhsT, rhs, start, stop, reads, writes):
        self.S.op("pe", lambda e: e.matmul(out, lhsT=lhsT, rhs=rhs, start=start, stop=stop),
                  reads=reads, writes=writes)

    def act(self, out, in_, func, reads, writes, **kw):
        self.S.op("act", lambda e: e.activation(out=out, in_=in_, func=func, **kw),
                  reads=reads, writes=writes)

    def tt(self, out, in0, in1, op, reads, writes, eng="dve"):
        self.S.op(eng, lambda e: e.tensor_tensor(out=out, in0=in0, in1=in1, op=op),
                  reads=reads, writes=writes)

    def stt(self, out, in0, scalar, in1, op0, op1, reads, writes, eng="dve"):
        self.S.op(eng, lambda e: e.scalar_tensor_tensor(out=out, in0=in0, scalar=scalar, in1=in1,
                                                         op0=op0, op1=op1),
                  reads=reads, writes=writes)

    def ts(self, out, in0, s1, s2, op0, op1, reads, writes, eng="dve"):
        if s2 is None:
            self.S.op(eng, lambda e: e.tensor_scalar(out=out, in0=in0, scalar1=s1, scalar2=None, op0=op0),
                      reads=reads, writes=writes)
        else:
            self.S.op(eng, lambda e: e.tensor_scalar(out=out, in0=in0, scalar1=s1, scalar2=s2,
                                                     op0=op0, op1=op1),
                      reads=reads, writes=writes)

    def dma(self, out, in_, reads, writes, stream, eng="sp"):
        if stream == "misc" or stream is None:
            self.uid += 1
            stream = ("u", self.uid)
        self.S.op(eng, lambda e: e.dma_start(out=out, in_=in_), reads=reads, writes=writes,
                  stream=stream)

    def setup(self, T):
        nc = self.nc
        self.T = T
        self.NT = T // 128
        self.tch = [(t0, min(512, T - t0)) for t0 in range(0, T, 512)]
        TA = T
        self.x = self.sb("x", [128, KC, T], F32)
        W_ = min(512, T)
        self.arena = self.sb("arena", [128, max(3 * KC * TA, 92 * W_)], BF16)
        n = KC * TA
        self.h = self.arena[:, 0:n].rearrange("p (k t) -> p k t", k=KC)
        self.outs = self.arena[:, n:2 * n].rearrange("p (k t) -> p k t", k=KC)
        self.m = self.arena[:, 2 * n:3 * n].rearrange("p (k t) -> p k t", k=KC)
        self.y = self.arena[:, 0:2 * n].bitcast(F32).rearrange("p (k t) -> p k t", k=KC)
        self.wslots = [self.sb("wslot%d" % i, [128, 4096], BF16) for i in range(3)]
        self.ps = [self.es.enter_context(nc.psum_tensor("ps%d" % i, [128, 512], F32)) for i in range(8)]
        self.vec = self.sb("vec", [128, NVEC], F32)
        self.mod = self.sb("mod", [128, 96, 2], F32)
        self.sil = self.sb("sil", [128, KC, 2], BF16)
        self.coef = self.sb("coef", [128, 6, KC], F32)
        self.rstd = self.sb("rstd", [128, T], F32)
        self.tmpA = self.sb("tmpA", [128, 512], F32)
        self.tmpB = self.sb("tmpB", [128, 512], F32)
        self.sqb = self.sb("sqb", [128, 512], BF16)
        self.ones = self.sb("ones", [128, 128], BF16)
        self.pm = self.sb("pm", [128, 128], BF16)
        self.epsD = self.sb("epsD", [128, 1], F32)
        self.eps128 = self.sb("eps128", [128, 1], F32)
        self.eps1 = self.sb("eps1", [128, 1], F32)
        self.S.op("dve", lambda e: e.memset(self.ones[:], 1.0), writes=["ones"])
        self.S.op("dve", lambda e: e.memset(self.epsD[:], D * EPS), writes=["epsD"])
        self.S.op("dve", lambda e: e.memset(self.eps128[:], EPS), writes=["eps128"])
        self.S.op("dve", lambda e: e.memset(self.eps1[:], EPS), writes=["eps1"])
        pm_d = self.inp("pm", [128, 128])
        self.dma(self.pm[:], pm_d, [], ["pm"], "misc", eng="pool")

    def vcol(self, name, j=None):
        a, b = VEC_COLS[name]
        if j is None:
            return self.vec[:, a:b]
        return self.vec[:, a + j:a + j + 1]

    def wload(self, dram_ap, nelem):
        i = self.wrot % 3
        self.wrot += 1
        slot = self.wslots[i]
        key = ("w", i)
        self.dma(slot[:, 0:nelem], dram_ap, [], [key], key, eng="pool")
        return slot, key

    def acc_pair(self):
        i = self.accrot % 2
        self.accrot += 1
        return [self.ps[2 * i], self.ps[2 * i + 1]], [("ps", 2 * i), ("ps", 2 * i + 1)]

    def linear_fm(self, wt, groups, kc_n, gc, rhs_fn, evac, tch=None):
        tch = tch or self.tch
        for g in groups:
            slot, wkey = self.wload(wt[g], kc_n * gc)
            sv = slot[:, 0:kc_n * gc].rearrange("p (k c) -> p k c", k=kc_n)
            for jj in range(gc // 128):
                j = g * (gc // 128) + jj
                accs, akeys = self.acc_pair()
                outs_ = []
                for ti, (t0, tn) in enumerate(tch):
                    for kc in range(kc_n):
                        rhs, rkeys = rhs_fn(kc, t0, tn)
                        self.mm(accs[ti][:, 0:tn], sv[:, kc, jj * 128:(jj + 1) * 128], rhs,
                                kc == 0, kc == kc_n - 1, [wkey] + rkeys, [akeys[ti]])
                    outs_.append(accs[ti][:, 0:tn])
                evac(j, outs_, akeys)

    def linear_tm(self, wt, groups, kc_n, gc, lhs_fn, evac):
        for g in groups:
            slot, wkey = self.wload(wt[g], kc_n * gc)
            sv = slot[:, 0:kc_n * gc].rearrange("p (k c) -> p k c", k=kc_n)
            for tt_ in range(self.NT):
                i = self.accrot % 4
                self.accrot += 1
                acc = self.ps[i]
                akey = ("ps", i)
                for kc in range(kc_n):
                    lhs, lkeys = lhs_fn(kc, tt_)
                    self.mm(acc[:, 0:gc], lhs, sv[:, kc, :], kc == 0, kc == kc_n - 1,
                            [wkey] + lkeys, [akey])
                evac(g, tt_, acc[:, 0:gc], akey)

    def front_mod(self, L, col):
        S = self.S
        vec_d = self.inp("vec%d" % L, [128, NVEC])
        self.dma(self.vec[:], vec_d, [], ["vec"], "misc")
        self.act(self.sil[:, :, 0], self.vcol("c"), AF.Silu, ["vec"], ["sil"])
        self.act(self.sil[:, :, 1], self.vcol("cctx"), AF.Silu, ["vec"], ["sil"])
        import os
        lvl = int(os.environ.get("FM_LVL", "9"))
        if lvl < 1:
            return
        adaw = self.inp("adaw%d" % L, [48, 128, 4096])
        modps = self.ps[4]
        mp = modps[:, 0:192].rearrange("p (j c) -> p j c", c=2)
        for g in range(48):
            slot, wkey = self.wload(adaw[g], 4096)
            sv = slot[:, :].rearrange("p (k c) -> p k c", k=KC)
            for jj in range(2):
                j = 2 * g + jj
                for kc in range(KC):
                    self.mm(mp[:, j, :], sv[:, kc, jj * 128:(jj + 1) * 128], self.sil[:, kc, :],
                            kc == 0, kc == KC - 1, [wkey, "sil"], [("ps", 4)])
        if lvl < 2:
            return
        a, b = VEC_COLS["ada_b"]
        for c in range(2):
            self.tt(self.mod[:, :, c], mp[:, :, c], self.vec[:, a:b], ALU.add,
                    [("ps", 4), "vec"], ["mod"])
        if lvl < 3:
            return
        self.coefs(col)

    def coefs(self, col):
        sD = float(np.sqrt(D))
        md = self.mod
        cf = self.coef
        import os
        lvl = int(os.environ.get("FM_LVL", "99"))
        self.ts(cf[:, 0, :], md[:, 16:32, col], 1.0, sD, ALU.add, ALU.mult, ["mod"], ["coef"])
        if lvl < 4: return
        self.tt(cf[:, 0, :], cf[:, 0, :], self.vcol("npm"), ALU.mult, ["coef", "vec"], ["coef"])
        if lvl < 5: return
        self.S.op("dve", lambda e: e.tensor_copy(out=cf[:, 1, :], in_=md[:, 0:16, col]), reads=["mod"], writes=["coef"])
        if lvl < 6: return
        self.stt(cf[:, 2, :], md[:, 32:48, col], sD, self.vcol("npo"), ALU.mult, ALU.mult, ["mod", "vec"], ["coef"])
        if lvl < 7: return
        self.ts(cf[:, 3, :], md[:, 64:80, col], 1.0, sD, ALU.add, ALU.mult, ["mod"], ["coef"])
        self.tt(cf[:, 3, :], cf[:, 3, :], self.vcol("npf"), ALU.mult, ["coef", "vec"], ["coef"])
        self.S.op("dve", lambda e: e.tensor_copy(out=cf[:, 4, :], in_=md[:, 48:64, col]), reads=["mod"], writes=["coef"])
        self.stt(cf[:, 5, :], md[:, 80:96, col], sD, self.vcol("npof"), ALU.mult, ALU.mult, ["mod", "vec"], ["coef"])

    def rstd_from(self, src_fn, t0, tn, nk):
        ssp = self.ps[5]
        for k in range(nk):
            src, keys = src_fn(k)
            self.act(self.sqb[:, 0:tn], src, AF.Square, keys, ["sqb"])
            self.mm(ssp[:, 0:tn], self.ones[:], self.sqb[:, 0:tn], k == 0, k == nk - 1,
                    ["ones", "sqb"], [("ps", 5)])
        self.rstd_fin(ssp, t0, tn)

    def rstd_fin(self, ssp, t0, tn, key=("ps", 5)):
        self.act(self.tmpA[:, 0:tn], ssp[:, 0:tn], AF.Ln, [key, "epsD"], ["tmpA"], bias=self.epsD[:, 0:1])
        self.act(self.rstd[:, t0:t0 + tn], self.tmpA[:, 0:tn], AF.Exp, ["tmpA"], [("rstd", t0)], scale=-0.5)

    def make_h(self, dst, ai, bi, t0, tn, dt0=0):
        for kc in range(KC):
            tmp = self.tmpA if kc % 2 == 0 else self.tmpB
            tk = "tmpA" if kc % 2 == 0 else "tmpB"
            self.stt(tmp[:, 0:tn], self.x[:, kc, t0:t0 + tn], self.coef[:, ai, kc:kc + 1],
                     self.rstd[:, t0:t0 + tn], ALU.mult, ALU.mult,
                     [("x", kc), "coef", ("rstd", t0)], [tk])
            self.act(dst[:, kc, dt0:dt0 + tn], tmp[:, 0:tn], AF.Identity, [tk, "coef"], [("h", kc)],
                     bias=self.coef[:, bi, kc:kc + 1])

    def front_h(self):
        for (t0, tn) in self.tch:
            self.rstd_from(lambda k: (self.x[:, k, t0:t0 + tn], [("x", k)]), t0, tn, KC)
            self.make_h(self.h, 0, 1, t0, tn, dt0=t0)

    def h_rhs(self, kc, t0, tn):
        return self.h[:, kc, t0:t0 + tn], [("h", kc)]

    def h_lhs(self, kc, tt_):
        return self.h[:, kc, tt_ * 128:(tt_ + 1) * 128], [("h", kc)]

    def qknorm(self, acc, akey, tn, gname, dst, dkeys, rope_t0=None):
        self.act(self.sqb[:, 0:tn], acc, AF.Square, [akey], ["sqb"])
        ssp = self.ps[5]
        self.mm(ssp[:, 0:tn], self.ones[:], self.sqb[:, 0:tn], True, True, ["ones", "sqb"], [("ps", 5)])
        self.act(self.tmpA[:, 0:tn], ssp[:, 0:tn], AF.Ln, [("ps", 5), "eps128"], ["tmpA"],
                 bias=self.eps128[:, 0:1], scale=1.0 / 128)
        self.act(self.tmpB[:, 0:tn], self.tmpA[:, 0:tn], AF.Exp, ["tmpA"], ["tmpB"], scale=-0.5)
        if rope_t0 is None:
            self.stt(dst, acc, self.vcol(gname, 0), self.tmpB[:, 0:tn], ALU.mult, ALU.mult,
                     [akey, "vec", "tmpB"], dkeys)
            return
        qnb = self.qnb
        self.stt(qnb[:, 0:tn], acc, self.vcol(gname, 0), self.tmpB[:, 0:tn], ALU.mult, ALU.mult,
                 [akey, "vec", "tmpB"], ["qnb"])
        pq = self.ps[6]
        self.mm(pq[:, 0:tn], self.pm[:], qnb[:, 0:tn], True, True, ["pm", "qnb"], [("ps", 6)])
        self.tt(self.tmpA[:, 0:tn], qnb[:, 0:tn], self.cosT[:, rope_t0:rope_t0 + tn], ALU.mult,
                ["qnb", "rope"], ["tmpA"])
        self.tt(self.tmpB[:, 0:tn], pq[:, 0:tn], self.sinT[:, rope_t0:rope_t0 + tn], ALU.mult,
                [("ps", 6), "rope"], ["tmpB"])
        self.tt(dst, self.tmpA[:, 0:tn], self.tmpB[:, 0:tn], ALU.add, ["tmpA", "tmpB"], dkeys)

    def kv_stage(self, L, tokset, kaT_d, va_d, kdT_d, vd_d, pb_d):
        T = self.T
        win = self.inp("win%d" % L, [48, 128, 4096])
        rope = tokset == "lat"
        stg = self.stg
        stgf = self.stgf

        def ev_ak(j, accs, akeys):
            hd = j - 4
            for ti, (t0, tn) in enumerate(self.tch):
                self.qknorm(accs[ti], akeys[ti], tn, "akn", stg[:, t0:t0 + tn], [("stg", t0)],
                            rope_t0=t0 if rope else None)
            self.dma(kaT_d[hd], stg[:, :], [("stg", t0) for t0, _ in self.tch], [("d", "kaT")], "kvout")
        self.linear_fm(win, [2], KC, 256, self.h_rhs, ev_ak)

        def ev_tm(dst_d, width):
            def ev(g, tt_, acc, akey):
                c0 = (g % 2) * 256 if width == 512 else 0
                sl = self.stg[:, 0:256]
                k = ("stg", 0)
                self.act(sl, acc, AF.Copy, [akey], [k])
                self.dma(dst_d[tt_ * 128:(tt_ + 1) * 128, c0:c0 + 256], sl, [k], [("d", "tm")], "kvout")
            return ev
        self.linear_tm(win, [3], KC, 256, self.h_lhs, ev_tm(va_d, 256))

        def ev_b(j, accs, akeys):
            g = j - 8
            for ti, (t0, tn) in enumerate(self.tch):
                self.act(stgf[:, t0:t0 + tn], accs[ti], AF.Copy, [akeys[ti]], [("stgf", t0)])
            self.dma(pb_d[g], stgf[:, :], [("stgf", t0) for t0, _ in self.tch], [("d", "pb")], "kvout_f")
        self.linear_fm(win, [4, 5], KC, 256, self.h_rhs, ev_b)

        def ev_dk(j, accs, akeys):
            hd = j - 24
            for ti, (t0, tn) in enumerate(self.tch):
                self.act(stg[:, t0:t0 + tn], accs[ti], AF.Copy, [akeys[ti]], [("stg", t0)])
            self.dma(kdT_d[hd], stg[:, :], [("stg", t0) for t0, _ in self.tch], [("d", "kdT")], "kvout")
        self.linear_fm(win, [12, 13], KC, 256, self.h_rhs, ev_dk)
        self.linear_tm(win, [14, 15], KC, 256, self.h_lhs, ev_tm(vd_d, 512))

    def attention(self, groups, nq_total, key_blocks, out_chunk0):
        kbuf = self.kbuf
        vbuf = self.vbuf
        rot = 0
        for qc, (q0, nq) in enumerate(self.tch):
            for kvi, heads in groups:
                blocks = key_blocks(kvi, qc)
                first = True
                nblk = len(blocks)
                for bi, blk in enumerate(blocks):
                    nk = blk["nk"]
                    r = rot % 2
                    rot += 1
                    kb = kbuf[r]
                    vb = vbuf[r]
                    self.dma(kb[:, 0:nk], blk["k"], [], [("kb", r)], ("kb", r))
                    self.dma(vb[:, 0:nk // 128, :], blk["v"].rearrange("(t p) d -> p t d", p=128),
                             [], [("vb", r)], ("vb", r))
                    for kt in range(nk // 128):
                        last = (bi == nblk - 1) and (kt == nk // 128 - 1)
                        for hi, h in enumerate(heads):
                            si = self.srot % 2
                            self.srot += 1
                            sps = self.ps[6 + si]
                            skey = ("ps", 6 + si)
                            self.mm(sps[:, 0:nq], kb[:, kt * 128:(kt + 1) * 128], self.qT[h][:, q0:q0 + nq],
                                    True, True, [("kb", r), ("qT", h)], [skey])
                            pT = self.pT[si]
                            pkey = ("pT", si)
                            if blk.get("bias") is not None:
                                bt = self.btile[si]
                                bkey = ("bt", si)
                                self.dma(bt[:, 0:nq], blk["bias"](h, kt), [], [bkey], bkey)
                                tf = self.tmpA if si == 0 else self.tmpB
                                tk = "tmpA" if si == 0 else "tmpB"
                                self.stt(tf[:, 0:nq], sps[:, 0:nq], SCALE, bt[:, 0:nq], ALU.mult, ALU.add,
                                         [skey, bkey], [tk])
                                self.act(pT[:, 0:nq], tf[:, 0:nq], AF.Exp, [tk], [pkey])
                            else:
                                self.act(pT[:, 0:nq], sps[:, 0:nq], AF.Exp, [skey], [pkey], scale=SCALE)
                            ops_ = self.ps[hi]
                            dps = self.ps[2 + hi]
                            self.mm(ops_[:, 0:nq], vb[:, kt, :], pT[:, 0:nq], first, last,
                                    [("vb", r), pkey], [("ps", hi)])
                            self.mm(dps[:, 0:nq], self.ones[:], pT[:, 0:nq], first, last,
                                    ["ones", pkey], [("ps", 2 + hi)])
                        first = False
                for hi, h in enumerate(heads):
                    tf = self.tmpA if hi == 0 else self.tmpB
                    tk = "tmpA" if hi == 0 else "tmpB"
                    self.S.op("dve", lambda e, tf=tf, hi=hi: e.reciprocal(out=tf[:, 0:nq], in_=self.ps[2 + hi][:, 0:nq]),
                              reads=[("ps", 2 + hi)], writes=[tk])
                    self.tt(self.outs[:, out_chunk0 + h, q0:q0 + nq], self.ps[hi][:, 0:nq], tf[:, 0:nq],
                            ALU.mult, [("ps", hi), tk], [("o", out_chunk0 + h)])

    def rest_stage(self, L, tokset, kv):
        T = self.T
        NT = self.NT
        lat = tokset == "lat"
        import os
        self.rest_lvl = int(os.environ.get("REST_LVL", "99"))
        win = self.inp("win%d" % L, [48, 128, 4096])
        n = KC * T
        mreg = self.arena[:, 2 * n:3 * n]
        self.qT = [mreg[:, i * T:(i + 1) * T] for i in range(4)]
        qdT = [mreg[:, (4 + i) * T:(5 + i) * T] for i in range(4)]
        vraw = mreg[:, 8 * T:16 * T].bitcast(F32).rearrange("p (t c) -> p t c", c=512)
        vn = self.arena[:, n:n + 4 * T].rearrange("p (t c) -> p t c", c=512)

        def ev_aq(j, accs, akeys):
            for ti, (t0, tn) in enumerate(self.tch):
                self.qknorm(accs[ti], akeys[ti], tn, "aqn", self.qT[j][:, t0:t0 + tn], [("qT", j)],
                            rope_t0=t0 if lat else None)
        self.linear_fm(win, [0, 1], KC, 256, self.h_rhs, ev_aq)

        if self.rest_lvl < 2:
            return
        def ev_cv(g, tt_, acc, akey):
            c0 = (g - 8) * 256
            self.act(vraw[:, tt_, c0:c0 + 256], acc, AF.Copy, [akey], [("vraw", tt_)])
        self.linear_tm(win, [8, 9], KC, 256, self.h_lhs, ev_cv)
        cb = self.cbc
        st = self.lnst
        for tt_ in range(NT):
            v = vraw[:, tt_, :]
            rk = [("vraw", tt_)]
            self.S.op("dve", lambda e, v=v: e.reduce_sum(out=st[:, 0:1], in_=v, axis=AX.X), reads=rk, writes=["lnst"])
            self.tt(self.tmpA[:, :], v, v, ALU.mult, rk, ["tmpA"])
            self.S.op("dve", lambda e: e.reduce_sum(out=st[:, 1:2], in_=self.tmpA[:, :], axis=AX.X), reads=["tmpA"], writes=["lnst"])
            self.ts(st[:, 2:3], st[:, 0:1], 1.0 / 512, None, ALU.mult, None, ["lnst"], ["lnst"])
            self.tt(st[:, 3:4], st[:, 2:3], st[:, 2:3], ALU.mult, ["lnst"], ["lnst"])
            self.stt(st[:, 4:5], st[:, 1:2], 1.0 / 512, st[:, 3:4], ALU.mult, ALU.subtract, ["lnst"], ["lnst"])
            self.act(st[:, 5:6], st[:, 4:5], AF.Ln, ["lnst", "eps1"], ["lnst"], bias=self.eps1[:, 0:1])
            self.act(st[:, 6:7], st[:, 5:6], AF.Exp, ["lnst"], ["lnst"], scale=-0.5)
            self.ts(self.tmpA[:, :], v, st[:, 2:3], st[:, 6:7], ALU.subtract, ALU.mult, rk + ["lnst"], ["tmpA"])
            self.tt(self.tmpB[:, :], self.tmpA[:, :], cb[:, 0, :], ALU.mult, ["tmpA", "cbc"], ["tmpB"])
            self.tt(vn[:, tt_, :], self.tmpB[:, :], cb[:, 1, :], ALU.add, ["tmpB", "cbc"], [("vn", tt_)])

        if self.rest_lvl < 3:
            return
        def ev_cu(j, accs, akeys):
            gi = j - 12
            for ti, (t0, tn) in enumerate(self.tch):
                for k in range(tn // 128):
                    tt_ = (t0 // 128) + k
                    mp = self.ps[5]
                    self.mm(mp[:, 0:128], vn[:, tt_, gi * 128:(gi + 1) * 128], self.wsT[:, gi, :], True, True,
                            [("vn", tt_), "wsT"], [("ps", 5)])
                    self.tt(self.tmpA[:, 0:128], mp[:, 0:128], cb[:, 2, gi * 128:(gi + 1) * 128], ALU.add,
                            [("ps", 5), "cbc"], ["tmpA"])
                    self.tt(self.outs[:, 8 + gi, t0 + k * 128:t0 + (k + 1) * 128], self.tmpA[:, 0:128],
                            accs[ti][:, k * 128:(k + 1) * 128], ALU.mult, ["tmpA", akeys[ti]], [("o", 8 + gi)])
        self.linear_fm(win, [6, 7], KC, 256, self.h_rhs, ev_cu)

        if self.rest_lvl < 4:
            return
        def ev_dq(j, accs, akeys):
            hd = j - 20
            for ti, (t0, tn) in enumerate(self.tch):
                self.act(qdT[hd][:, t0:t0 + tn], accs[ti], AF.Copy, [akeys[ti]], [("qdT", hd)])
        self.linear_fm(win, [10, 11], KC, 256, self.h_rhs, ev_dq)

        if self.rest_lvl < 5:
            return
        self.S.barrier()
        hreg = self.arena[:, 0:n] if T == TL else self.mixreg[:, :]
        self.kbuf = [hreg[:, i * 1024:(i + 1) * 1024] for i in range(2)]
        self.vbuf = [hreg[:, 2048 + i * 1024:2048 + (i + 1) * 1024].rearrange("p (t d) -> p t d", d=128) for i in range(2)]
        self.pT = [hreg[:, 4096 + i * 512:4096 + (i + 1) * 512] for i in range(2)]
        self.btile = [hreg[:, 5120 + i * 1024:5120 + (i + 1) * 1024].bitcast(F32) for i in range(2)]
        pbuf = hreg[:, 7168:7168 + 2 * (T + 16)].bitcast(F32)
        wbuf = [hreg[:, 7168 + (2 + 2 * i) * (T + 16):7168 + (4 + 2 * i) * (T + 16)].bitcast(F32) for i in range(2)]
        dlt = hreg[:, 7168 + 6 * (T + 16):7168 + 6 * (T + 16) + T]
        self.srot = 0

        def a_blocks(kvi, qc):
            bl = []
            if lat:
                for r in range(NCORE):
                    bl.append(dict(k=kv["kaT_all"][r, kvi * 128:(kvi + 1) * 128, :],
                                   v=kv["va_all"][r, :, kvi * 128:(kvi + 1) * 128], nk=1024))
            bl.append(dict(k=kv["kaT_c"][kvi], v=kv["va_c"][:, kvi * 128:(kvi + 1) * 128], nk=CTX))
            return bl
        self.attention([(0, [0, 1]), (1, [2, 3])], T, a_blocks, 0)

        if self.rest_lvl < 6:
            return
        save_qT = self.qT
        self.qT = qdT

        def d_blocks(kvi, qc):
            bl = []
            if lat:
                bt = kv["dbias"]
                bl.append(dict(k=kv["kdT_win"][kvi, :, qc * 512:qc * 512 + 1024],
                               v=kv["vd_win"][qc * 512:qc * 512 + 1024, kvi * 128:(kvi + 1) * 128], nk=1024,
                               bias=lambda h, kt, qc=qc: bt[h, qc, kt]))
            bl.append(dict(k=kv["kdT_c"][kvi], v=kv["vd_c"][:, kvi * 128:(kvi + 1) * 128], nk=CTX))
            return bl
        self.attention([(i, [i]) for i in range(4)], T, d_blocks, 12)
        self.qT = save_qT

        if self.rest_lvl < 7:
            return
        pb_d = kv["pb"]
        halo_d = kv["pb_halo"]
        inv_d = self.inp("invcnt_" + tokset, [4, 128, T])
        for g in range(4):
            self.dma(pbuf[:, 8:8 + T], pb_d[g], [], ["pbuf"], "pb")
            self.dma(pbuf[:, 0:8], halo_d[g][:, 0:8], [], ["pbuf"], "pb")
            self.dma(pbuf[:, 8 + T:16 + T], halo_d[g][:, 8:16], [], ["pbuf"], "pb")
            self.dma(self.invt[:, 0:T], inv_d[g], [], ["invt"], "invt")
            NB = T + 16
            src = pbuf
            skey = "pbuf"
            lo, hi, sh = 1, NB, None
            steps = [(1, 0, 1, NB)]
            if g >= 1:
                steps.append((1, 1, 2, NB - 1))
            if g >= 2:
                steps.append((2, 2, 4, NB - 3))
            if g >= 3:
                steps.append((4, 4, 8, NB - 8))
            for si, (sl, sr, lo, hi) in enumerate(steps):
                dstb = wbuf[si % 2]
                dkey = ("wbuf", si % 2)
                self.tt(dstb[:, lo:hi], src[:, lo - sl:hi - sl], src[:, lo + sr:hi + sr], ALU.add,
                        [skey], [dkey])
                src = dstb
                skey = dkey
            oth = wbuf[(len(steps)) % 2]
            okey = ("wbuf", len(steps) % 2)
            self.tt(oth[:, 8:8 + T], src[:, 8:8 + T], self.invt[:, 0:T], ALU.mult, [skey, "invt"], [okey])
            self.tt(dlt[:, 0:T], oth[:, 8:8 + T], pbuf[:, 8:8 + T], ALU.subtract, [okey, "pbuf"], ["dlt"])
            for ti, (t0, tn) in enumerate(self.tch):
                acc = self.ps[4]
                self.mm(acc[:, 0:tn], self.bw[:, g, :], dlt[:, t0:t0 + tn], True, True, ["bw", "dlt"], [("ps", 4)])
                self.act(self.outs[:, 4 + g, t0:t0 + tn], acc[:, 0:tn], AF.Copy, [("ps", 4), "vec"], [("o", 4 + g)],
                         scale=self.vcol("bsc", g))

        if self.rest_lvl < 8:
            return
        self.S.barrier()
        for (t0, tn) in self.tch:
            self.make_h(self.h, 0, 1, t0, tn, dt0=t0)

        if self.rest_lvl < 9:
            return
        wm = self.inp("wm%d" % L, [64, 128, 2560])
        macc = self.macc
        for j in range(KC):
            for i in range(4):
                slot, wkey = self.wload(wm[j * 4 + i], 2560)
                sv = slot[:, 0:2560].rearrange("p (k c) -> p k c", k=20)
                gps, gkeys = [self.ps[0], self.ps[1]], [("ps", 0), ("ps", 1)]
                yps, ykeys = [self.ps[2], self.ps[3]], [("ps", 2), ("ps", 3)]
                if (j * 4 + i) % 2 == 1:
                    gps, gkeys = [self.ps[4], self.ps[5]], [("ps", 4), ("ps", 5)]
                    yps, ykeys = [self.ps[6], self.ps[7]], [("ps", 6), ("ps", 7)]
                for ti, (t0, tn) in enumerate(self.tch):
                    for kc in range(KC):
                        self.mm(gps[ti][:, 0:tn], sv[:, kc, :], self.h[:, kc, t0:t0 + tn], kc == 0, kc == KC - 1,
                                [wkey, ("h", kc)], [gkeys[ti]])
                    for kc in range(4):
                        self.mm(yps[ti][:, 0:tn], sv[:, 16 + kc, :], self.outs[:, 4 * i + kc, t0:t0 + tn],
                                kc == 0, kc == 3, [wkey, ("o", 4 * i + kc)], [ykeys[ti]])
                    tf = self.tmpA if ti == 0 else self.tmpB
                    tk = "tmpA" if ti == 0 else "tmpB"
                    self.act(tf[:, 0:tn], gps[ti][:, 0:tn], AF.Sigmoid, [gkeys[ti]], [tk])
                    if i == 0:
                        self.tt(macc[:, t0:t0 + tn], tf[:, 0:tn], yps[ti][:, 0:tn], ALU.mult, [tk, ykeys[ti]], [("macc", t0)])
                    else:
                        self.tt(tf[:, 0:tn], tf[:, 0:tn], yps[ti][:, 0:tn], ALU.mult, [tk, ykeys[ti]], [tk])
                        dst = self.m[:, j, t0:t0 + tn] if i == 3 else macc[:, t0:t0 + tn]
                        dk = ("m", j) if i == 3 else ("macc", t0)
                        self.tt(dst, macc[:, t0:t0 + tn], tf[:, 0:tn], ALU.add, [("macc", t0), tk], [dk], eng="pool")

        if self.rest_lvl < 10:
            return
        self.S.barrier()
        wo = self.inp("wo%d" % L, [8, 128, 4096])
        ssp = [self.ps[4], self.ps[5]]

        def ev_wo(j, accs, akeys):
            for ti, (t0, tn) in enumerate(self.tch):
                sq = self.sqb if ti == 0 else self.sqb2
                sk = "sqb" if ti == 0 else "sqb2"
                self.S.op("dve", lambda e, ti=ti, t0=t0, tn=tn, j=j, a=accs[ti]: e.tensor_copy(out=self.y[:, j, t0:t0 + tn], in_=a),
                          reads=[akeys[ti]], writes=[("y", j)])
                self.act(sq[:, 0:tn], self.y[:, j, t0:t0 + tn], AF.Square, [("y", j)], [sk])
                self.mm(ssp[ti][:, 0:tn], self.ones[:], sq[:, 0:tn], j == 0, j == KC - 1, ["ones", sk], [("ps", 4 + ti)])
        self.linear_fm(wo, range(8), KC, 256, lambda kc, t0, tn: (self.m[:, kc, t0:t0 + tn], [("m", kc)]), ev_wo)
        for ti, (t0, tn) in enumerate(self.tch):
            self.rstd_fin(ssp[ti], t0, tn, key=("ps", 4 + ti))
        for j in range(KC):
            for ti, (t0, tn) in enumerate(self.tch):
                tf = self.tmpA if ti == 0 else self.tmpB
                tk = "tmpA" if ti == 0 else "tmpB"
                self.stt(tf[:, 0:tn], self.y[:, j, t0:t0 + tn], self.coef[:, 2, j:j + 1], self.rstd[:, t0:t0 + tn],
                         ALU.mult, ALU.mult, [("y", j), "coef", ("rstd", t0)], [tk])
                self.tt(self.x[:, j, t0:t0 + tn], self.x[:, j, t0:t0 + tn], tf[:, 0:tn], ALU.add,
                        [("x", j), tk], [("x", j)], eng="pool")
        self.S.barrier()

        if self.rest_lvl < 11:
            return
        wgu = self.inp("wgu%d" % L, [HC, 128, 4096])
        wd = self.inp("wd%d" % L, [32, 128, 2816])
        ar = self.arena
        for (t0, tn) in self.tch:
            W_ = min(512, T)
            hf = ar[:, 0:KC * W_].rearrange("p (k t) -> p k t", k=KC)
            actb = ar[:, KC * W_:(KC + HC) * W_].rearrange("p (k t) -> p k t", k=HC)
            yf = ar[:, (KC + HC) * W_:(KC + HC) * W_ + 2 * KC * W_].bitcast(F32).rearrange("p (k t) -> p k t", k=KC)
            self.rstd_from(lambda k: (self.x[:, k, t0:t0 + tn], [("x", k)]), t0, tn, KC)
            self.make_h(hf, 3, 4, t0, tn, dt0=0)

            def ev_gu(j, accs, akeys, tn=tn):
                hc = j // 2
                if j % 2 == 0:
                    self.act(self.silt[:, 0:tn], accs[0], AF.Silu, [akeys[0]], ["silt"])
                else:
                    self.tt(actb[:, hc, 0:tn], self.silt[:, 0:tn], accs[0], ALU.mult, ["silt", akeys[0]], [("act", hc)])
            self.linear_fm(wgu, range(HC), KC, 256, lambda kc, a, b: (hf[:, kc, 0:b], [("h", kc)]), ev_gu, tch=[(0, tn)])
            ssq = self.ps[5]
            for j in range(KC):
                sl0, k0 = self.wload(wd[2 * j], 2816)
                sl1, k1 = self.wload(wd[2 * j + 1], 2816)
                accs, akeys = self.acc_pair()
                acc = accs[0]
                ak = akeys[0]
                for half, (sl, wk) in enumerate(((sl0, k0), (sl1, k1))):
                    sv = sl[:, 0:2816].rearrange("p (k c) -> p k c", k=22)
                    for kk in range(22):
                        hc = half * 22 + kk
                        self.mm(acc[:, 0:tn], sv[:, kk, :], actb[:, hc, 0:tn], hc == 0, hc == HC - 1,
                                [wk, ("act", hc)], [ak])
                self.S.op("dve", lambda e, j=j, acc=acc, tn=tn: e.tensor_copy(out=yf[:, j, 0:tn], in_=acc[:, 0:tn]),
                          reads=[ak], writes=[("yf", j)])
                self.act(self.sqb[:, 0:tn], yf[:, j, 0:tn], AF.Square, [("yf", j)], ["sqb"])
                self.mm(ssq[:, 0:tn], self.ones[:], self.sqb[:, 0:tn], j == 0, j == KC - 1, ["ones", "sqb"], [("ps", 5)])
            self.rstd_fin(ssq, t0, tn)
            for j in range(KC):
                tf = self.tmpA if j % 2 == 0 else self.tmpB
                tk = "tmpA" if j % 2 == 0 else "tmpB"
                self.stt(tf[:, 0:tn], yf[:, j, 0:tn], self.coef[:, 5, j:j + 1], self.rstd[:, t0:t0 + tn],
                         ALU.mult, ALU.mult, [("yf", j), "coef", ("rstd", t0)], [tk])
                self.tt(self.x[:, j, t0:t0 + tn], self.x[:, j, t0:t0 + tn], tf[:, 0:tn], ALU.add,
                        [("x", j), tk], [("x", j)], eng="pool")
            self.S.barrier()

    def load_layer_consts(self, L):
        cbc_d = self.inp("cbc%d" % L, [128, 3, 512])
        self.dma(self.cbc[:], cbc_d, [], ["cbc"], "misc")
        self.dma(self.wsT[:], self.inp("wsT%d" % L, [128, 4, 128]), [], ["wsT"], "misc", eng="pool")
        self.dma(self.bw[:], self.inp("bw%d" % L, [128, 4, 128]), [], ["bw"], "misc", eng="pool")

    def alloc_mixer_state(self, T):
        self.qnb = self.sb("qnb", [128, 512], BF16)
        self.stg = self.sb("stg", [128, T], BF16)
        self.stgf = self.sb("stgf", [128, T], F32)
        self.macc = self.stgf
        n_ = KC * T
        if T == TL:
            self.cbc = self.arena[:, n_ + 4 * T:n_ + 4 * T + 3072].bitcast(F32).rearrange("p (a c) -> p a c", a=3)
            self.invt = self.arena[:, 2 * n_ + 8 * T:2 * n_ + 10 * T].bitcast(F32)
            self.ropet = self.arena[:, n_ + 12 * T:n_ + 16 * T].bitcast(F32).rearrange("p (a t) -> p a t", a=2)
        else:
            self.cbc = self.sb("cbc", [128, 3, 512], F32)
            self.invt = self.sb("invt", [128, T], F32)
            self.mixreg = self.sb("mixreg", [128, 16384], BF16)
        self.wsT = self.sb("wsT", [128, 4, 128], BF16)
        self.bw = self.sb("bw", [128, 4, 128], BF16)
        self.lnst = self.sb("lnst", [128, 8], F32)
        self.sqb2 = self.sb("sqb2", [128, 512], BF16)
        self.silt = self.stgf[:, 0:min(512, T)]

    def load_x(self, name):
        xin = self.inp(name, [D, self.T])
        for kc in range(KC):
            self.dma(self.x[:, kc, :], xin[kc * 128:(kc + 1) * 128, :], [], [("x", kc)], ("xio", kc))

    def store_x(self, name):
        xo = self.outp(name, [D, self.T])
        for kc in range(KC):
            self.dma(xo[kc * 128:(kc + 1) * 128, :], self.x[:, kc, :], [("x", kc)], [("d", "xo", kc)], ("xio", kc))

    def dbg(self, name, ap, shape, keys, dt=F32):
        o = self.outp("dbg_" + name, shape, dt)
        self.dma(o, ap, keys, [("d", "dbg", name)], None)

    def finish(self):
        self.S.emit(self.nc)
        self.es.close()
        return self.nc


def kv_dram(P, pref, T, out):
    f = P.outp if out else P.inp
    return dict(kaT=f(pref + "kaT", [2, 128, T], BF16), va=f(pref + "va", [T, 256], BF16),
                kdT=f(pref + "kdT", [4, 128, T], BF16), vd=f(pref + "vd", [T, 512], BF16),
                pb=f(pref + "pb", [4, 128, T], F32))


def build_ctx_prog(stop=99):
    P = Prog("ctx")
    T = CTX
    P.setup(T)
    P.alloc_mixer_state(T)
    P.cosT = None
    P.load_x("xT")
    kv0 = kv_dram(P, "c0_", T, True)
    kv1 = kv_dram(P, "c1_", T, True)
    if stop >= 1:
        P.front_mod(0, 1)
    if stop == 1:
        P.dbg("mod", P.mod[:, :, :], [128, 96, 2], ["mod"])
        P.dbg("coef", P.coef[:, :, :], [128, 6, KC], ["coef"])
    if stop >= 2:
        P.front_h()
    if stop == 2:
        P.dbg("rstd", P.rstd[:, :], [128, T], [("rstd", 0)])
        P.dbg("h", P.h, [128, KC, T], [("h", k) for k in range(KC)], BF16)
    if stop <= 2:
        P.store_x("xo")
        return P.finish(), P
    P.load_layer_consts(0)
    P.kv_stage(0, "ctx", kv0["kaT"], kv0["va"], kv0["kdT"], kv0["vd"], kv0["pb"])
    if stop == 3:
        return P.finish(), P
    kv = dict(kaT_c=kv0["kaT"], va_c=kv0["va"], kdT_c=kv0["kdT"], vd_c=kv0["vd"], pb=kv0["pb"],
              pb_halo=P.inp("pb_halo", [4, 128, 16]))
    P.rest_stage(0, "ctx", kv)
    if stop == 4:
        P.dbg("outs", P.outs, [128, KC, T], [], BF16)
    if stop == 5:
        P.store_x("xo")
        return P.finish(), P
    P.front_mod(1, 1)
    P.front_h()
    P.kv_stage(1, "ctx", kv1["kaT"], kv1["va"], kv1["kdT"], kv1["vd"], kv1["pb"])
    return P.finish(), P


def build_lat_kv_prog(L):
    P = Prog("latkv")
    T = TL
    P.setup(T)
    P.alloc_mixer_state(T)
    rope = P.inp("rope", [2, 128, T])
    P.cosT = P.ropet[:, 0, :]
    P.sinT = P.ropet[:, 1, :]
    P.dma(P.ropet[:, 0, :], rope[0], [], ["rope"], "misc")
    P.dma(P.ropet[:, 1, :], rope[1], [], ["rope"], "misc")
    P.load_x("xT")
    kvo = kv_dram(P, "o_", T, True)
    P.front_mod(L, 0)
    P.front_h()
    P.kv_stage(L, "lat", kvo["kaT"], kvo["va"], kvo["kdT"], kvo["vd"], kvo["pb"])
    return P.finish(), P


def build_lat_rest_prog(L):
    P = Prog("latrest")
    T = TL
    P.setup(T)
    P.alloc_mixer_state(T)
    rope = P.inp("rope", [2, 128, T])
    P.cosT = P.ropet[:, 0, :]
    P.sinT = P.ropet[:, 1, :]
    P.dma(P.ropet[:, 0, :], rope[0], [], ["rope"], "misc")
    P.dma(P.ropet[:, 1, :], rope[1], [], ["rope"], "misc")
    P.load_x("xT")
    P.front_mod(L, 0)
    P.front_h()
    P.load_layer_consts(L)
    kv = dict(kaT_all=P.inp("kaT_all", [NCORE, 256, TL], BF16), va_all=P.inp("va_all", [NCORE, TL, 256], BF16),
              kaT_c=P.inp("kaT_c", [2, 128, CTX], BF16), va_c=P.inp("va_c", [CTX, 256], BF16),
              kdT_win=P.inp("kdT_win", [4, 128, 1536], BF16), vd_win=P.inp("vd_win", [1536, 512], BF16),
              kdT_c=P.inp("kdT_c", [4, 128, CTX], BF16), vd_c=P.inp("vd_c", [CTX, 512], BF16),
              dbias=P.inp("dbias", [4, 2, 8, 128, 512]),
              pb=P.inp("pb", [4, 128, TL]), pb_halo=P.inp("pb_halo", [4, 128, 16]))
    P.rest_stage(L, "lat", kv)
    P.store_x("xo")
    return P.finish(), P


def tile_w(W, gc):
    K, N = W.shape
    kc = K // 128
    G = N // gc
    return np.ascontiguousarray(W.reshape(kc, 128, G, gc).transpose(2, 1, 0, 3)).reshape(G, 128, kc * gc)


def fm(v):
    return np.ascontiguousarray(v.reshape(-1, 128).T)


def layer_inputs(inp, L):
    d = {}
    vec = np.zeros((128, NVEC), np.float32)
    def put(name, arr):
        a, b = VEC_COLS[name]
        vec[:, a:b] = arr
    put("c", fm(inp["c"][0]))
    put("cctx", fm(inp["c_ctx"]))
    put("ada_b", fm(inp["ada_b"][L]))
    put("npm", fm(inp["norm_pre_mix"][L]))
    put("npo", fm(inp["norm_post_mix"][L]))
    put("npf", fm(inp["norm_pre_ffn"][L]))
    put("npof", fm(inp["norm_post_ffn"][L]))
    put("aqn", inp["a_q_norm"][L].reshape(128, 1))
    put("akn", inp["a_k_norm"][L].reshape(128, 1))
    put("bsc", fm(inp["b_scale"][L]))
    d["vec%d" % L] = vec
    d["adaw%d" % L] = tile_w(inp["ada_w"][L], 256)
    d["win%d" % L] = tile_w(inp["w_in"][L], 256)
    w = inp["w_in"][L]
    G0 = 4096
    wg = w[:, G0:].reshape(KC, 128, 4, KC, 128)
    wg = wg.transpose(3, 2, 1, 0, 4)
    wb = inp["w_br"][L].reshape(4, 4, 128, KC, 128)
    wb = wb.transpose(3, 0, 2, 1, 4)
    wm = np.concatenate([wg, wb], axis=3)
    d["wm%d" % L] = np.ascontiguousarray(wm).reshape(64, 128, 2560)
    d["wo%d" % L] = tile_w(inp["w_o"][L], 256)
    wgt = inp["w_gate"][L].reshape(KC, 128, HC, 128)
    wut = inp["w_up"][L].reshape(KC, 128, HC, 128)
    wgu = np.concatenate([wgt, wut], axis=3)
    d["wgu%d" % L] = np.ascontiguousarray(wgu.transpose(2, 1, 0, 3)).reshape(HC, 128, 4096)
    wdn = inp["w_down"][L].reshape(2, 22, 128, KC, 128)
    d["wd%d" % L] = np.ascontiguousarray(wdn.transpose(3, 0, 2, 1, 4)).reshape(32, 128, 2816)
    cbc = np.zeros((128, 3, 512), np.float32)
    cbc[:, 0, :] = inp["c_norm_g"][L][None, :]
    cbc[:, 1, :] = inp["c_norm_b"][L][None, :]
    cbc[:, 2, :] = inp["c_bs"][L].reshape(1, 512)
    d["cbc%d" % L] = cbc
    d["wsT%d" % L] = np.ascontiguousarray(inp["c_ws"][L].transpose(2, 0, 1))
    d["bw%d" % L] = np.ascontiguousarray(inp["b_w"][L].transpose(1, 0, 2))
    return d


def perm_matrix():
    pm = np.zeros((128, 128), np.float32)
    for m_ in range(128):
        pm[m_ ^ 32, m_] = 1.0
    return pm


def rope_tables(core):
    t = np.arange(core * TL, (core + 1) * TL)
    row = (t // GRID_W).astype(np.float32)
    col = (t % GRID_W).astype(np.float32)
    inv = (10000.0 ** (-np.arange(32, dtype=np.float32) / 32)).astype(np.float32)
    out = np.zeros((2, 128, TL), np.float32)
    for d_ in range(128):
        axis = d_ // 64
        pair = (d_ % 64) // 32
        f = d_ % 32
        ang = (row if axis == 0 else col) * inv[f]
        out[0, d_] = np.cos(ang)
        out[1, d_] = np.sin(ang) * (-1.0 if pair == 0 else 1.0)
    return out


def invcnt(n, t_lo, t_hi):
    t = np.arange(t_lo, t_hi)
    out = np.zeros((4, 128, t_hi - t_lo), np.float32)
    for g, w in enumerate((2, 4, 8, 16)):
        lo = np.clip(t - w // 2, 0, n)
        hi = np.clip(t + w // 2, 0, n)
        out[g] = (1.0 / (hi - lo).astype(np.float32))[None, :]
    return out


def dbias_table(rpb, core):
    out = np.full((4, 2, 8, 128, 512), NEG, np.float32)
    qi = np.arange(512)
    for b in range(2):
        qr = 16 * core + 8 * b + qi // 64
        qc = qi % 64
        r0 = np.clip(qr - 4, 0, 128 - 8)
        c0 = np.clip(qc - 8, 0, 64 - 16)
        for tl in range(8):
            kw = (4 * b + tl) * 128 + np.arange(128)
            kr = 16 * core + kw // 64 - 4
            kc = kw % 64
            KR = kr[:, None]
            KCc = kc[:, None]
            ok = (KR >= r0[None, :]) & (KR < r0[None, :] + 8) & (KCc >= c0[None, :]) & (KCc < c0[None, :] + 16) \
                & (KR >= 0) & (KR < 128)
            dr = np.clip(KR - qr[None, :] + 7, 0, 14)
            dc = np.clip(KCc - qc[None, :] + 15, 0, 30)
            for h in range(4):
                vals = rpb[h][dr, dc]
                out[h, b, tl] = np.where(ok, vals, np.float32(NEG))
    return out


_PROGS = {}


def _prog(key, fn):
    if key not in _PROGS:
        _PROGS[key] = fn()
    return _PROGS[key]


def kernel(**inputs):
    inp = {k: np.asarray(v) for k, v in inputs.items()}
    cores = list(range(NCORE))
    pm = perm_matrix()
    lay = [layer_inputs(inp, L) for L in range(2)]
    zeros_halo = np.zeros((4, 128, 16), np.float32)

    def pick(P, pool):
        return {k: pool[k] for k in P.din}

    nc_c, Pc = _prog("ctx", build_ctx_prog)
    pool = dict(lay[0])
    pool.update(lay[1])
    pool.update(pm=pm, xT=np.ascontiguousarray(inp["ctx"][0].T), pb_halo=zeros_halo,
                invcnt_ctx=invcnt(CTX, 0, CTX))
    res = run_bass_kernel_spmd(nc_c, [pick(Pc, pool) for _ in cores], core_ids=cores)
    cres = res.results[0]
    ckv = [{k[3:]: cres[k] for k in cres if k.startswith("c%d_" % L)} for L in range(2)]

    x_sh = [np.ascontiguousarray(inp["x"][0, c * TL:(c + 1) * TL].T) for c in cores]
    ropes = [rope_tables(c) for c in cores]
    for L in range(2):
        nc_a, Pa = _prog(("kv", L), lambda: build_lat_kv_prog(L))
        maps = []
        for c in cores:
            pool = dict(lay[L])
            pool.update(pm=pm, xT=x_sh[c], rope=ropes[c])
            maps.append(pick(Pa, pool))
        ra = run_bass_kernel_spmd(nc_a, maps, core_ids=cores).results
        kaT_all = np.stack([ra[c]["o_kaT"].reshape(256, TL) for c in cores])
        va_all = np.stack([ra[c]["o_va"] for c in cores])
        nc_b, Pb = _prog(("rest", L), lambda: build_lat_rest_prog(L))
        maps = []
        for c in cores:
            kd = ra[c]["o_kdT"]
            vd = ra[c]["o_vd"]
            pb = ra[c]["o_pb"]
            kd_prev = ra[c - 1]["o_kdT"][:, :, 768:] if c > 0 else np.zeros_like(kd[:, :, :256])
            kd_next = ra[c + 1]["o_kdT"][:, :, :256] if c < NCORE - 1 else np.zeros_like(kd[:, :, :256])
            vd_prev = ra[c - 1]["o_vd"][768:] if c > 0 else np.zeros_like(vd[:256])
            vd_next = ra[c + 1]["o_vd"][:256] if c < NCORE - 1 else np.zeros_like(vd[:256])
            halo = np.zeros((4, 128, 16), np.float32)
            if c > 0:
                halo[:, :, 0:8] = ra[c - 1]["o_pb"][:, :, TL - 8:]
            if c < NCORE - 1:
                halo[:, :, 8:16] = ra[c + 1]["o_pb"][:, :, 0:8]
            pool = dict(lay[L])
            pool.update(pm=pm, xT=x_sh[c], rope=ropes[c], kaT_all=kaT_all, va_all=va_all,
                        kaT_c=ckv[L]["kaT"], va_c=ckv[L]["va"], kdT_c=ckv[L]["kdT"], vd_c=ckv[L]["vd"],
                        kdT_win=np.concatenate([kd_prev, kd, kd_next], axis=2),
                        vd_win=np.concatenate([vd_prev, vd, vd_next], axis=0),
                        dbias=dbias_table(inp["d_rpb"][L], c), pb=pb, pb_halo=halo,
                        invcnt_lat=invcnt(SEQ, c * TL, (c + 1) * TL))
            maps.append(pick(Pb, pool))
        rb = run_bass_kernel_spmd(nc_b, maps, core_ids=cores).results
        x_sh = [np.ascontiguousarray(rb[c]["xo"]) for c in cores]
    out = np.concatenate([x_sh[c].T for c in cores], axis=0)[None]
    return np.ascontiguousarray(out.astype(np.float32))
```
